# Optimizing a Trainium2 kernel written in Bass

```python
import jax, jax.numpy as jnp
from jax import lax
import numpy as np

D_MODEL = 2048
BATCH = 4
SEQ = 4096
DEPTH = 4

MEM_LEN = 256
G_CHUNK = 128
G_GROUPS = 8
G_GROUP_DIM = D_MODEL // 16
G_WIDTH = G_GROUPS * G_GROUP_DIM
M_HEADS = 4
M_QK_DIM = D_MODEL // 8
M_V_DIM = D_MODEL // 4
M_QK_WIDTH = M_HEADS * M_QK_DIM
M_V_WIDTH = M_HEADS * M_V_DIM
M_CHUNK = 64
M_CONV = 4
X_HEADS = 4
X_HEAD_DIM = D_MODEL // 8
X_WIDTH = X_HEADS * X_HEAD_DIM
N_BRANCH = 3
D_FF = ((-(-8 * D_MODEL // 3) + 255) // 256) * 256
ALPHA = (2 * DEPTH) ** 0.25
BETA = (8 * DEPTH) ** -0.25
IN_SPLITS = (G_WIDTH, G_WIDTH, 2 * M_QK_WIDTH, M_V_WIDTH, M_V_WIDTH,
             M_HEADS, M_HEADS, X_WIDTH, N_BRANCH * D_MODEL)
IN_WIDTH = sum(IN_SPLITS)
IN_OFFSETS = tuple(int(o) for o in np.cumsum(IN_SPLITS)[:-1])
F_GATE_OFFSET = sum(IN_SPLITS[:6])
LN_EPS = 1e-5

kernel_name = 'gmlp_mlstm_memxattn_deepnorm_hybrid'


def layer_norm(x, g, b):
    xf = x.astype(jnp.float32)
    mu = jnp.mean(xf, -1, keepdims=True)
    var = jnp.mean(jnp.square(xf - mu), -1, keepdims=True)
    return ((xf - mu) * lax.rsqrt(var + LN_EPS) * g + b).astype(x.dtype)


def head_norm(h):
    hf = h.astype(jnp.float32)
    mu = jnp.mean(hf, -1, keepdims=True)
    var = jnp.mean(jnp.square(hf - mu), -1, keepdims=True)
    return (hf - mu) * lax.rsqrt(var + LN_EPS)


def causal_depthwise_conv(x, w, b):
    k = w.shape[0]
    y = lax.conv_general_dilated(x, w[:, None, :], window_strides=(1,),
                                 padding=[(k - 1, 0)],
                                 dimension_numbers=('NWC', 'WIO', 'NWC'),
                                 feature_group_count=x.shape[-1])
    return y + b


def chunked_spatial_gating(u, v, ln_g, ln_b, w_s, b_s):
    B, S, _ = v.shape
    v = layer_norm(v, ln_g, ln_b)
    vc = v.reshape(B, S // G_CHUNK, G_CHUNK, G_GROUPS, G_GROUP_DIM)
    causal = jnp.tril(jnp.ones((G_CHUNK, G_CHUNK), dtype=bool))
    w = jnp.where(causal, w_s, 0)
    mixed = jnp.einsum('gts,bcsgd->bctgd', w, vc) + b_s.T[None, None, :, :, None]
    return u * mixed.reshape(B, S, G_WIDTH)


def mlstm_chunkwise(q, k, v, i_pre, f_pre):
    B, S, H, dk = q.shape
    dv = v.shape[-1]
    L = M_CHUNK
    NC = S // L
    f32 = jnp.float32

    def to_chunks(t):
        t = t.astype(f32).reshape((B, NC, L, H) + t.shape[3:])
        return jnp.moveaxis(t, (1, 3), (0, 2))

    qc = to_chunks(q)
    kc = to_chunks(k) * (dk ** -0.5)
    vc = to_chunks(v)
    ic = to_chunks(i_pre)
    lfc = to_chunks(jax.nn.log_sigmoid(f_pre.astype(f32)))
    causal = jnp.tril(jnp.ones((L, L), dtype=bool))

    def step(carry, xs):
        C, n, m = carry
        qb, kb, vb, ib, lfb = xs
        b = jnp.cumsum(lfb, axis=-1)
        D = jnp.where(causal, b[..., :, None] - b[..., None, :] + ib[..., None, :], -jnp.inf)
        m_inter = b + m[..., None]
        m_row = jnp.maximum(jnp.max(D, -1), m_inter)
        P = jnp.exp(D - m_row[..., None]) * jnp.einsum('bhtk,bhsk->bhts', qb, kb)
        w_inter = jnp.exp(m_inter - m_row)
        num = (jnp.einsum('bhts,bhsv->bhtv', P, vb)
               + w_inter[..., None] * jnp.einsum('bhtk,bhkv->bhtv', qb, C))
        den = jnp.sum(P, -1) + w_inter * jnp.einsum('bhtk,bhk->bht', qb, n)
        h = num / jnp.maximum(jnp.abs(den), jnp.exp(-m_row))[..., None]
        b_last = b[..., -1]
        g = b_last[..., None] - b + ib
        m_new = jnp.maximum(b_last + m, jnp.max(g, -1))
        w_state = jnp.exp(g - m_new[..., None])
        decay = jnp.exp(b_last + m - m_new)
        C = decay[..., None, None] * C + jnp.einsum('bhs,bhsk,bhsv->bhkv', w_state, kb, vb)
        n = decay[..., None] * n + jnp.einsum('bhs,bhsk->bhk', w_state, kb)
        return (C, n, m_new), h

    init = (jnp.zeros((B, H, dk, dv), f32), jnp.zeros((B, H, dk), f32), jnp.zeros((B, H), f32))
    _, h = lax.scan(step, init, (qc, kc, vc, ic, lfc))
    return jnp.moveaxis(h, (0, 2), (1, 3)).reshape(B, S, H, dv)


def memory_cross_attention(q, k, v):
    s = jnp.einsum('bshd,bmhd->bhsm', q, k).astype(jnp.float32) * (q.shape[-1] ** -0.5)
    p = jax.nn.softmax(s, axis=-1).astype(v.dtype)
    return jnp.einsum('bhsm,bmhd->bshd', p, v)


def token_mixer(x, mem_n, w_in, b_in, g_ln_g, g_ln_b, g_ws, g_bs, m_conv_w, m_conv_b,
                m_norm_g, x_w_kv, w_pa, w_pb, w_pc, w_out):
    B, S, _ = x.shape
    z = x @ w_in + b_in
    zu, zv, zqk, zvm, zo, zi, zf, zqx, zg = jnp.split(z, IN_OFFSETS, axis=-1)
    y_a = chunked_spatial_gating(jax.nn.gelu(zu), jax.nn.gelu(zv), g_ln_g, g_ln_b, g_ws, g_bs)
    qk = jax.nn.silu(causal_depthwise_conv(zqk, m_conv_w, m_conv_b))
    q_m, k_m = jnp.split(qk, 2, axis=-1)
    h = mlstm_chunkwise(q_m.reshape(B, S, M_HEADS, M_QK_DIM), k_m.reshape(B, S, M_HEADS, M_QK_DIM),
                        zvm.reshape(B, S, M_HEADS, M_V_DIM), zi, zf)
    h = head_norm(h).reshape(B, S, M_V_WIDTH) * m_norm_g
    y_b = (jax.nn.sigmoid(zo) * h).astype(x.dtype)
    k_x, v_x = jnp.split(mem_n @ x_w_kv, 2, axis=-1)
    y_c = memory_cross_attention(zqx.reshape(B, S, X_HEADS, X_HEAD_DIM),
                                 k_x.reshape(B, -1, X_HEADS, X_HEAD_DIM),
                                 v_x.reshape(B, -1, X_HEADS, X_HEAD_DIM)).reshape(B, S, X_WIDTH)
    gates = jax.nn.sigmoid(zg).reshape(B, S, N_BRANCH, D_MODEL)
    merged = (gates[:, :, 0] * (y_a @ w_pa) + gates[:, :, 1] * (y_b @ w_pb)
              + gates[:, :, 2] * (y_c @ w_pc))
    return merged @ w_out


def swiglu_ffn(x, w_gu, w_down):
    gate, up = jnp.split(x @ w_gu, 2, axis=-1)
    return (jax.nn.silu(gate) * up) @ w_down


def setup_inputs(seed: int = 0) -> dict:
    key = jax.random.key(seed)
    keys = iter(jax.random.split(key, 32))
    L = DEPTH

    def nrm(shape, scale):
        return scale * jax.random.normal(next(keys), shape, jnp.float32)

    x = nrm((BATCH, SEQ, D_MODEL), 1.0)
    mem = nrm((BATCH, MEM_LEN, D_MODEL), 1.0)
    mem_ln_g = 1.0 + nrm((D_MODEL,), 0.02)
    mem_ln_b = nrm((D_MODEL,), 0.02)
    w_in = nrm((L, D_MODEL, IN_WIDTH), D_MODEL ** -0.5)
    b_in = nrm((L, IN_WIDTH), 0.02)
    f_bias = jnp.linspace(3.0, 6.0, M_HEADS, dtype=jnp.float32)
    b_in = b_in.at[:, F_GATE_OFFSET:F_GATE_OFFSET + M_HEADS].add(f_bias)
    g_ln_g = 1.0 + nrm((L, G_WIDTH), 0.02)
    g_ln_b = nrm((L, G_WIDTH), 0.02)
    g_ws = nrm((L, G_GROUPS, G_CHUNK, G_CHUNK), G_CHUNK ** -0.5)
    g_bs = 1.0 + nrm((L, G_GROUPS, G_CHUNK), 0.02)
    m_conv_w = nrm((L, M_CONV, 2 * M_QK_WIDTH), M_CONV ** -0.5)
    m_conv_b = nrm((L, 2 * M_QK_WIDTH), 0.02)
    m_norm_g = 1.0 + nrm((L, M_V_WIDTH), 0.02)
    x_w_kv = nrm((L, D_MODEL, 2 * X_WIDTH), D_MODEL ** -0.5)
    w_pa = nrm((L, G_WIDTH, D_MODEL), G_WIDTH ** -0.5)
    w_pb = nrm((L, M_V_WIDTH, D_MODEL), M_V_WIDTH ** -0.5)
    w_pc = nrm((L, X_WIDTH, D_MODEL), X_WIDTH ** -0.5)
    w_out = nrm((L, D_MODEL, D_MODEL), BETA * D_MODEL ** -0.5)
    ln1_g = 1.0 + nrm((L, D_MODEL), 0.02)
    ln1_b = nrm((L, D_MODEL), 0.02)
    w_gu = nrm((L, D_MODEL, 2 * D_FF), D_MODEL ** -0.5)
    w_down = nrm((L, D_FF, D_MODEL), BETA * D_FF ** -0.5)
    ln2_g = 1.0 + nrm((L, D_MODEL), 0.02)
    ln2_b = nrm((L, D_MODEL), 0.02)
    return {'x': x, 'mem': mem, 'mem_ln_g': mem_ln_g, 'mem_ln_b': mem_ln_b,
            'w_in': w_in, 'b_in': b_in, 'g_ln_g': g_ln_g, 'g_ln_b': g_ln_b,
            'g_ws': g_ws, 'g_bs': g_bs, 'm_conv_w': m_conv_w, 'm_conv_b': m_conv_b,
            'm_norm_g': m_norm_g, 'x_w_kv': x_w_kv, 'w_pa': w_pa, 'w_pb': w_pb,
            'w_pc': w_pc, 'w_out': w_out, 'ln1_g': ln1_g, 'ln1_b': ln1_b,
            'w_gu': w_gu, 'w_down': w_down, 'ln2_g': ln2_g, 'ln2_b': ln2_b}


def reference(x, mem, mem_ln_g, mem_ln_b, w_in, b_in, g_ln_g, g_ln_b, g_ws, g_bs,
              m_conv_w, m_conv_b, m_norm_g, x_w_kv, w_pa, w_pb, w_pc, w_out,
              ln1_g, ln1_b, w_gu, w_down, ln2_g, ln2_b):
    mem_n = layer_norm(mem, mem_ln_g, mem_ln_b)
    for l in range(DEPTH):
        mix = token_mixer(x, mem_n, w_in[l], b_in[l], g_ln_g[l], g_ln_b[l], g_ws[l], g_bs[l],
                          m_conv_w[l], m_conv_b[l], m_norm_g[l], x_w_kv[l],
                          w_pa[l], w_pb[l], w_pc[l], w_out[l])
        x = layer_norm(ALPHA * x + mix, ln1_g[l], ln1_b[l])
        x = layer_norm(ALPHA * x + swiglu_ffn(x, w_gu[l], w_down[l]), ln2_g[l], ln2_b[l])
    return x
```

```python
import contextlib
import numpy as np
import concourse.bass as bass
import concourse.mybir as mybir
from concourse.bass_utils import run_bass_kernel_spmd

F32 = mybir.dt.float32
BF16 = mybir.dt.bfloat16
AF = mybir.ActivationFunctionType
ALU = mybir.AluOpType
AX = mybir.AxisListType

EPOCH = 30000


class Buf:
    __slots__ = ("name", "w", "r")

    def __init__(self, name):
        self.name = name
        self.w = None
        self.r = {}


def bufs(name, n):
    return [Buf("%s%d" % (name, i)) for i in range(n)]


class Eng:
    def __init__(self, prog, name):
        self.name = name
        self.ops = []
        self.known = {}
        self.sem = prog.new_sem(name)
        self.tick = 0


class Prog:
    def __init__(self):
        self.nsem = 0
        self.sem_names = []
        self.engs = {n: Eng(self, n) for n in ("pe", "act", "dve", "pool", "sp")}
        self.n_ops = 0
        self.fence_tokens = []

    def new_sem(self, name):
        self.sem_names.append("%s_%d" % (name, self.nsem))
        self.nsem += 1
        return self.nsem - 1

    def _need(self, e, toks):
        need = {}
        for (sem, val, en) in toks:
            if en == e.name and e.name == "pe":
                continue
            if e.known.get(sem, 0) >= val:
                continue
            if need.get(sem, 0) < val:
                need[sem] = val
        for sem, val in need.items():
            e.known[sem] = val
            e.ops.append(("wait", sem, val))

    def _waits(self, e, reads, writes):
        deps = []
        for b in reads:
            if b.w is not None:
                deps.append(b.w)
        for b in writes:
            if b.w is not None:
                deps.append(b.w)
            deps.extend(b.r.values())
        self._need(e, deps)

    def op(self, eng, fn, reads=(), writes=(), signal=True):
        e = self.engs[eng]
        self._waits(e, reads, writes)
        if e.tick >= EPOCH:
            e.sem = self.new_sem(e.name)
            e.tick = 0
        if signal:
            e.tick += 1
            tok = (e.sem, e.tick, e.name)
            e.ops.append(("op", fn, e.sem, 1))
        else:
            tok = (e.sem, e.tick + 1, e.name)
            e.ops.append(("op", fn, None, 0))
        for b in writes:
            b.w = tok
            b.r = {}
        for b in reads:
            if b not in writes:
                b.r[e.name] = tok
        self.n_ops += 1
        return tok

    def dma(self, queue, fn, slot, reads=(), writes=(), fence=True):
        e = self.engs[queue]
        self._waits(e, reads, writes)
        if slot["cnt"] > 0:
            self._need(e, [(slot["sem"], slot["cnt"], "dmaprev")])
        slot["cnt"] += 16
        tok = (slot["sem"], slot["cnt"], "dma%d" % slot["sem"])
        e.ops.append(("op", fn, slot["sem"], 16))
        for b in writes:
            b.w = tok
            b.r = {}
        for b in reads:
            if b not in writes:
                b.r[tok[2]] = tok
        if fence:
            self.fence_tokens.append(tok)
        self.n_ops += 1
        return tok

    def new_slot(self, name):
        return {"sem": self.new_sem(name), "cnt": 0}

    def barrier(self, engines=("pe", "act", "dve", "sp")):
        toks = list(self.fence_tokens)
        for n in engines:
            e = self.engs[n]
            if e.tick > 0:
                toks.append((e.sem, e.tick, "bar_" + n))
        for n in engines:
            self._need(self.engs[n], toks)
        self.fence_tokens = []

    def emit(self, nc):
        handles = {"pe": "tensor", "act": "scalar", "dve": "vector", "pool": "gpsimd", "sp": "sync"}
        with contextlib.ExitStack() as st:
            sems = [st.enter_context(nc.semaphore(n)) for n in self.sem_names]
            block = st.enter_context(nc.Block())
            for en, attr in handles.items():
                ops = self.engs[en].ops

                def body(h, ops=ops):
                    for o in ops:
                        if o[0] == "wait":
                            h.wait_ge(sems[o[1]], o[2])
                        else:
                            ins = o[1](h)
                            if o[2] is not None:
                                ins.then_inc(sems[o[2]], o[3])

                getattr(block, attr)(body)


D = 2048
KC = 16
T = 512
DEPTH = 4
MEM = 256
IN_W = 15368
OFF_U, OFF_V, OFF_QK, OFF_VM, OFF_O, OFF_I, OFF_F, OFF_QX, OFF_G = 0, 1024, 2048, 4096, 6144, 8192, 8196, 8200, 9224
DFF = 5632
ALPHA = float((2 * DEPTH) ** 0.25)
EPS = 1e-5
NPF = 280
P_CONVW, P_CONVB, P_MNG, P_L1G, P_L1B, P_L2G, P_L2B = 120, 184, 200, 216, 232, 248, 264
NPB = 3080
LN16 = float(np.log(16.0))


def fmchunk(col):
    return col // 128 if col < 8192 else (col - 8200) // 128 + 64


class Arena:
    def __init__(self, t, words):
        self.t = t
        self.off = 0
        self.words = words
        self.peak = 0

    def f32(self, n):
        ap = self.t[:, self.off:self.off + n]
        self.off += n
        self.peak = max(self.peak, self.off)
        assert self.off <= self.words, ("arena overflow", self.off, self.words)
        return ap

    def bf(self, n):
        w = (n + 1) // 2
        ap = self.t[:, self.off:self.off + w].bitcast(BF16)
        self.off += w
        self.peak = max(self.peak, self.off)
        assert self.off <= self.words, ("arena overflow", self.off, self.words)
        return ap


def r3(ap, inner):
    return ap.rearrange("p (a b) -> p a b", b=inner)


class _Stop(Exception):
    pass


def build(L, S, stop=None):
    NT = S // T
    dbg_list = []

    def stage(n, items):
        if stop == n:
            dbg_list.extend(items)
            raise _Stop()
    nc = bass.Bass("TRN2", target_bir_lowering=False)
    dt = lambda name, shape, kind="ExternalInput": nc.dram_tensor(name, shape, F32, kind=kind).ap()
    xfm = dt("xfm", [D, S])
    mem = dt("mem", [MEM, D])
    w_in = dt("w_in", [L, D, IN_W])
    x_w_kv = dt("x_w_kv", [L, D, 2048])
    w_pa = dt("w_pa", [L, 1024, D])
    w_pb = dt("w_pb", [L, 2048, D])
    w_pc = dt("w_pc", [L, 1024, D])
    w_out = dt("w_out", [L, D, D])
    w_gu = dt("w_gu", [L, D, 2 * DFF])
    w_down = dt("w_down", [L, DFF, D])
    prm_fm = dt("prm_fm", [L, 128, NPF])
    prm_bc = dt("prm_bc", [L, 128, NPB])
    brow = dt("brow", [L, 1, 3072])
    wsT_d = dt("wsT", [L, 128, 8, 128])
    wif_d = dt("wif", [L, 128, KC, 8])
    memg = dt("memg", [128, 32])
    cst = dt("cst", [128, 512])
    yfm = dt("yfm", [D, S], kind="ExternalOutput")
    dbg = dt("dbg", [128, 8192], kind="ExternalOutput") if stop is not None else None
    xscr = dt("xscr", [2, D, S], kind="Internal")

    P = Prog()
    st = contextlib.ExitStack()
    with st:
        ARW = 52000
        at = st.enter_context(nc.sbuf_tensor("arena", [128, ARW], F32))
        A = Arena(at, ARW)
        ps = [st.enter_context(nc.psum_tensor("ps%d" % i, [128, 512], F32)) for i in range(8)]
        b_ps = bufs("ps", 8)
        psb = [p[:].bitcast(BF16) for p in ps]

        NB = 3
        wr = [A.bf(KC * 512) for _ in range(NB)]
        b_wr = bufs("wr", NB)
        s_wr = [P.new_slot("wr%d" % i) for i in range(NB)]
        xb = r3(A.bf(KC * T), T)
        b_xb = bufs("xb", KC)
        Cst = A.f32(8 * 512)
        b_C = bufs("C", 4)
        nst = A.f32(8)
        b_n = bufs("n", 4)
        memnT = r3(A.bf(KC * MEM), MEM)
        b_memn = Buf("memn")
        kxT = r3(A.bf(8 * MEM), MEM)
        b_kx = Buf("kx")
        vx = r3(A.bf(2 * 1024), 1024)
        b_vx = Buf("vx")
        pfm = A.f32(NPF)
        b_pfm = Buf("pfm")
        bif = A.f32(8)
        b_bif = Buf("bif")
        browb = A.bf(3072)
        b_brow = Buf("brow")
        wsT = r3(A.bf(8 * 128), 128)
        b_wsT = Buf("wsT")
        wifb = r3(A.bf(KC * 8), 8)
        b_wif = Buf("wif")
        cstf = A.f32(512)
        b_cst = Buf("cst")
        identf = cstf[:, 0:128]
        maskf = cstf[:, 128:256]
        onesf = cstf[:, 256:384]
        cstb = A.bf(512)
        identb = cstb[:, 0:128]
        onesb = cstb[:, 256:384]
        b_cstb = Buf("cstb")
        meanwb = A.bf(128)
        b_meanw = Buf("meanw")
        halo = r3(A.f32(KC * 3), 3)
        b_halo = bufs("halo", KC)
        gstate = A.f32(8)
        b_gst = Buf("gst")
        memgs = A.f32(32)
        s_misc = P.new_slot("misc")
        s_x4 = [P.new_slot("x%d" % i) for i in range(4)]
        s_o4 = [P.new_slot("o%d" % i) for i in range(4)]
        s_o = s_o4[0]
        s_m = [P.new_slot("m%d" % i) for i in range(8)]
        base_mark = A.off

        def MM(out, lhsT, rhs, start, stop, reads, writes, signal=None):
            if signal is None:
                signal = stop
            P.op("pe", lambda h: h.matmul(out, lhsT=lhsT, rhs=rhs, start=start, stop=stop),
                 reads=reads, writes=writes, signal=signal)

        def TR(out, in_, ident, reads, writes, signal=True):
            P.op("pe", lambda h: h.transpose(out, in_, ident), reads=reads, writes=writes, signal=signal)

        def ACT(out, in_, func, reads, writes, bias=None, scale=None, accum=None):
            kw = {}
            if bias is not None:
                kw["bias"] = bias
            if scale is not None:
                kw["scale"] = scale
            if accum is not None:
                kw["accum_out"] = accum
            P.op("act", lambda h: h.activation(out=out, in_=in_, func=func, **kw), reads=reads, writes=writes)

        def TS(out, in0, s1, s2, op0, op1, reads, writes, eng="dve"):
            if op1 is None:
                P.op(eng, lambda h: h.tensor_scalar(out=out, in0=in0, scalar1=s1, scalar2=None, op0=op0),
                     reads=reads, writes=writes)
            else:
                P.op(eng, lambda h: h.tensor_scalar(out=out, in0=in0, scalar1=s1, scalar2=s2, op0=op0, op1=op1),
                     reads=reads, writes=writes)

        def TT(out, in0, in1, op, reads, writes, eng="dve"):
            P.op(eng, lambda h: h.tensor_tensor(out=out, in0=in0, in1=in1, op=op), reads=reads, writes=writes)

        def STT(out, in0, scalar, in1, op0, op1, reads, writes):
            P.op("dve", lambda h: h.scalar_tensor_tensor(out=out, in0=in0, scalar=scalar, in1=in1, op0=op0, op1=op1),
                 reads=reads, writes=writes)

        def CP(out, in_, reads, writes, eng="dve"):
            if eng == "act":
                ACT(out, in_, AF.Copy, reads, writes)
            else:
                P.op(eng, lambda h: h.tensor_copy(out=out, in_=in_), reads=reads, writes=writes)

        def DMA(q, out, in_, slot, reads=(), writes=(), fence=True):
            P.dma(q, lambda h: h.dma_start(out=out, in_=in_), slot, reads=reads, writes=writes, fence=fence)

        wstate = {"n": 0, "q": []}

        def wload(src_ap, nk, cw):
            wstate["q"].append((src_ap, nk, cw))

        def wissue():
            src_ap, nk, cw = wstate["q"].pop(0)
            i = wstate["n"] % NB
            wstate["n"] += 1
            dst = wr[i][:, 0:nk * cw].rearrange("p (k c) -> p k c", c=cw)
            DMA("pool", dst, src_ap.rearrange("(k p) c -> p k c", p=128), s_wr[i], writes=[b_wr[i]], fence=False)
            return i

        class WStream:
            def __init__(self):
                self.pending = []
                self.inflight = []

            def add(self, src, nk, cw):
                self.pending.append((src, nk, cw))

            def _pump(self):
                while self.pending and len(self.inflight) < NB - 1:
                    wload(*self.pending.pop(0))
                    self.inflight.append(wissue())

            def get(self):
                self._pump()
                i = self.inflight.pop(0)
                return i

            def after_get(self):
                self._pump()

        WS = WStream()

        def wview(i, nk, cw):
            return wr[i][:, 0:nk * cw].rearrange("p (k c) -> p k c", c=cw)

        ppc = {"n": 0}

        def ppbank():
            i = ppc["n"] % 4
            ppc["n"] += 1
            return i

        def fm_group(bank, wv, nk, c0, rhs3, rhs_bufs, wbuf, t0=0, tn=T):
            for k in range(nk):
                MM(ps[bank][:, 0:tn], wv[:, k, c0:c0 + 128], rhs3[:, k, t0:t0 + tn], k == 0, k == nk - 1,
                   reads=[wbuf] + rhs_bufs, writes=[b_ps[bank]])

        DMA("sp", cstf, cst, s_m[0], writes=[b_cst])
        CP(cstb, cstf, [b_cst], [b_cstb])
        P.op("dve", lambda h: h.memset(meanwb, 1.0 / D), writes=[b_meanw])
        DMA("sp", memgs, memg, s_m[1], writes=[b_memn])
        pm = A.off
        memt = r3(A.f32(2 * D), D)
        b_memt = Buf("memt")
        mstat = A.f32(64)
        b_mst = Buf("mst")
        DMA("sp", memt, mem.rearrange("(a p) d -> p a d", p=128), s_m[2], writes=[b_memt])
        for a in range(2):
            for q in range(4):
                P.op("dve", lambda h, a=a, q=q: h.bn_stats(out=mstat[:, q * 6:(q + 1) * 6], in_=memt[:, a, q * 512:(q + 1) * 512]),
                     reads=[b_memt], writes=[b_mst])
            P.op("dve", lambda h: h.bn_aggr(out=mstat[:, 24:26], in_=mstat[:, 0:24]), reads=[b_mst], writes=[b_mst])
            TS(mstat[:, 26:27], mstat[:, 25:26], EPS, None, ALU.add, None, [b_mst], [b_mst])
            ACT(mstat[:, 27:28], mstat[:, 26:27], AF.Sqrt, [b_mst], [b_mst])
            P.op("dve", lambda h: h.reciprocal(out=mstat[:, 28:29], in_=mstat[:, 27:28]), reads=[b_mst], writes=[b_mst])
            TS(memt[:, a, :], memt[:, a, :], mstat[:, 24:25], mstat[:, 28:29], ALU.subtract, ALU.mult, [b_mst, b_memt], [b_memt])
            for c in range(KC):
                bk = 4 + (c % 2)
                TR(ps[bk][:, 0:128], memt[:, a, c * 128:(c + 1) * 128], identf, [b_memt, b_cst], [b_ps[bk]])
                TS(memnT[:, c, a * 128:(a + 1) * 128], ps[bk][:, 0:128], memgs[:, c:c + 1], memgs[:, 16 + c:17 + c],
                   ALU.mult, ALU.add, [b_ps[bk], b_memn], [b_memn])
        P.barrier()
        A.off = pm

        try:
          stage(0, [(memnT.rearrange('p a b -> p (a b)'), 4096)])
          for l in range(L):
            DMA("sp", pfm, prm_fm[l], s_m[0], writes=[b_pfm])
            DMA("sp", bif, prm_bc[l, :, 3072:3080], s_m[1], writes=[b_bif])
            pm = A.off
            tmpf = A.f32(3072)
            b_tmpf = Buf("tmpf")
            DMA("sp", tmpf[0:1, :], brow[l], s_m[2], writes=[b_tmpf])
            CP(browb[0:1, :], tmpf[0:1, :], [b_tmpf], [b_brow])
            wst = r3(A.f32(1024), 128)
            b_wst = Buf("wst")
            DMA("sp", wst, wsT_d[l], s_m[3], writes=[b_wst])
            for g in range(8):
                TT(wsT[:, g, :], wst[:, g, :], maskf, ALU.mult, [b_wst, b_cst], [b_wsT])
            wiff = r3(A.f32(KC * 8), 8)
            b_wiff = Buf("wiff")
            DMA("sp", wiff, wif_d[l], s_m[4], writes=[b_wiff])
            CP(wifb, wiff, [b_wiff], [b_wif])
            P.op("dve", lambda h: h.memset(Cst, 0.0), writes=b_C)
            P.op("dve", lambda h: h.memset(nst, 0.0), writes=b_n)
            P.op("dve", lambda h: h.memset(gstate, 0.0), writes=[b_gst])
            P.op("dve", lambda h: h.memset(halo, 0.0), writes=b_halo)
            for cb in range(2):
                WS.add(x_w_kv[l, :, cb * 512:(cb + 1) * 512], KC, 512)
            for cb in range(2):
                WS.add(x_w_kv[l, :, 1024 + cb * 512:1024 + (cb + 1) * 512], KC, 512)
            for cb in range(2):
                wi = WS.get()
                wv = wview(wi, KC, 512)
                for c in range(4):
                    bk = ppbank()
                    fm_group(bk, wv, KC, c * 128, memnT, [b_memn], b_wr[wi], 0, MEM)
                    CP(kxT[:, cb * 4 + c, :], ps[bk][:, 0:MEM], [b_ps[bk]], [b_kx], eng="act")
                WS.after_get()
            for cb in range(2):
                wi = WS.get()
                wv = wview(wi, KC, 512)
                for mc in range(2):
                    bk = ppbank()
                    for k in range(KC):
                        MM(ps[bk][:, :], memnT[:, k, mc * 128:(mc + 1) * 128], wv[:, k, :], k == 0, k == KC - 1,
                           reads=[b_wr[wi], b_memn], writes=[b_ps[bk]])
                    CP(vx[:, mc, cb * 512:(cb + 1) * 512], ps[bk][:, :], [b_ps[bk]], [b_vx], eng="act")
                WS.after_get()
            P.barrier()
            A.off = pm

            stage(1, [(kxT.rearrange('p a b -> p (a b)'), 2048), (vx.rearrange('p a b -> p (a b)'), 2048)])
            src = xfm if l == 0 else xscr[(l - 1) % 2]
            dst = yfm if l == L - 1 else xscr[l % 2]

            for ti in range(NT):
                t0 = ti * T
                tile_mark = A.off
                R0 = A.f32(KC * T)
                s32 = r3(R0, T)
                b_s32 = bufs("s32", KC)
                R0b = R0.bitcast(BF16)
                yaT = r3(R0b[:, 0:8 * T], T)
                ycT = r3(R0b[:, 8 * T:16 * T], T)
                ybT = r3(R0b[:, 16 * T:32 * T], T)
                b_ya = bufs("ya", 8)
                b_yc = bufs("yc", 8)
                b_yb = bufs("yb", 16)
                r1_mark = A.off

                for q in range(4):
                    DMA("sp", s32[:, q * 4:(q + 1) * 4, :],
                        src[q * 512:(q + 1) * 512, t0:t0 + T].rearrange("(k p) t -> p k t", p=128),
                        s_x4[q], writes=b_s32[q * 4:(q + 1) * 4])
                for k in range(KC):
                    CP(xb[:, k, :], s32[:, k, :], [b_s32[k]], [b_xb[k]], eng=("dve" if k % 2 == 0 else "act"))
                P.barrier()
                A.off = r1_mark

                stage(2, [(xb.rearrange('p a b -> p (a b)'), 8192)])
                glg = A.f32(1024)
                glb = A.f32(1024)
                bsb = r3(A.f32(1024), 128)
                b_gl = Buf("gl")
                DMA("sp", glg, prm_bc[l, :, 0:1024], s_m[5], writes=[b_gl])
                DMA("sp", glb, prm_bc[l, :, 1024:2048], s_m[6], writes=[b_gl])
                DMA("sp", bsb.rearrange("p a b -> p (a b)"), prm_bc[l, :, 2048:3072], s_m[7], writes=[b_gl])
                uT = r3(A.bf(8 * T), T)
                b_u = bufs("u", 8)
                vtok = r3(A.bf(4 * 1024), 1024)
                b_vt = bufs("vt", 4)
                vg = [A.f32(1024), A.f32(1024)]
                b_vg = bufs("vg", 2)
                vstat = [A.f32(32), A.f32(32)]
                for cb in range(2):
                    WS.add(w_in[l, :, OFF_U + cb * 512:OFF_U + (cb + 1) * 512], KC, 512)
                for cb in range(2):
                    WS.add(w_in[l, :, OFF_V + cb * 512:OFF_V + (cb + 1) * 512], KC, 512)
                for cb in range(2):
                    wi = WS.get()
                    wv = wview(wi, KC, 512)
                    for c in range(4):
                        bk = ppbank()
                        ch = cb * 4 + c
                        fm_group(bk, wv, KC, c * 128, xb, b_xb, b_wr[wi])
                        ACT(uT[:, ch, :], ps[bk][:, :], AF.Gelu_apprx_tanh, [b_ps[bk], b_pfm], [b_u[ch]],
                            bias=pfm[:, fmchunk(OFF_U) + ch:fmchunk(OFF_U) + ch + 1])
                    WS.after_get()
                wi0 = WS.get()
                wi1 = WS.get()
                wvs = [wview(wi0, KC, 512), wview(wi1, KC, 512)]
                wbs = [b_wr[wi0], b_wr[wi1]]
                for s in range(4):
                    j = s % 2
                    for cb in range(2):
                        bk = ppbank()
                        for k in range(KC):
                            MM(ps[bk][:, :], xb[:, k, s * 128:(s + 1) * 128], wvs[cb][:, k, :], k == 0, False,
                               reads=[wbs[cb]] + b_xb, writes=[b_ps[bk]], signal=False)
                        MM(ps[bk][:, :], onesb[0:1, :], browb[0:1, cb * 512:(cb + 1) * 512], False, True,
                           reads=[b_cstb, b_brow], writes=[b_ps[bk]])
                        ACT(vg[j][:, cb * 512:(cb + 1) * 512], ps[bk][:, :], AF.Gelu_apprx_tanh, [b_ps[bk]], [b_vg[j]])
                    vs = vstat[j]
                    for q in range(2):
                        P.op("dve", lambda h, vs=vs, q=q, j=j: h.bn_stats(out=vs[:, q * 6:(q + 1) * 6], in_=vg[j][:, q * 512:(q + 1) * 512]),
                             reads=[b_vg[j]], writes=[b_vg[j]])
                    P.op("dve", lambda h, vs=vs: h.bn_aggr(out=vs[:, 12:14], in_=vs[:, 0:12]), reads=[b_vg[j]], writes=[b_vg[j]])
                    TS(vs[:, 14:15], vs[:, 13:14], EPS, None, ALU.add, None, [b_vg[j]], [b_vg[j]])
                    ACT(vs[:, 15:16], vs[:, 14:15], AF.Sqrt, [b_vg[j]], [b_vg[j]])
                    P.op("dve", lambda h, vs=vs: h.reciprocal(out=vs[:, 16:17], in_=vs[:, 15:16]), reads=[b_vg[j]], writes=[b_vg[j]])
                    TS(vg[j], vg[j], vs[:, 12:13], vs[:, 16:17], ALU.subtract, ALU.mult, [b_vg[j]], [b_vg[j]])
                    TT(vg[j], vg[j], glg, ALU.mult, [b_vg[j], b_gl], [b_vg[j]])
                    TT(vtok[:, s, :], vg[j], glb, ALU.add, [b_vg[j], b_gl], [b_vt[s]])
                WS.after_get()
                yat = [A.f32(T), A.f32(T)]
                b_yat = bufs("yat", 2)
                for g in range(8):
                    bk = 4 + (g % 2)
                    for s in range(4):
                        MM(ps[bk][:, s * 128:(s + 1) * 128], vtok[:, s, g * 128:(g + 1) * 128], wsT[:, g, :], True, True,
                           reads=[b_vt[s], b_wsT], writes=[b_ps[bk]], signal=(s == 3))
                    j = g % 2
                    for s in range(4):
                        TT(yat[j][:, s * 128:(s + 1) * 128], ps[bk][:, s * 128:(s + 1) * 128], bsb[:, g, :], ALU.add,
                           [b_ps[bk], b_gl], [b_yat[j]])
                    TT(yaT[:, g, :], yat[j], uT[:, g, :], ALU.mult, [b_yat[j], b_u[g]], [b_ya[g]])
                P.barrier()
                A.off = r1_mark

                stage(3, [(R0b[:, 0:8 * T], 4096)])
                qxT = r3(A.bf(8 * T), T)
                b_qx = bufs("qx", 8)
                for cb in range(2):
                    WS.add(w_in[l, :, OFF_QX + cb * 512:OFF_QX + (cb + 1) * 512], KC, 512)
                for cb in range(2):
                    wi = WS.get()
                    wv = wview(wi, KC, 512)
                    for c in range(4):
                        bk = ppbank()
                        ch = cb * 4 + c
                        fm_group(bk, wv, KC, c * 128, xb, b_xb, b_wr[wi])
                        ACT(qxT[:, ch, :], ps[bk][:, :], AF.Identity, [b_ps[bk], b_pfm], [b_qx[ch]],
                            bias=pfm[:, fmchunk(OFF_QX) + ch:fmchunk(OFF_QX) + ch + 1])
                    WS.after_get()
                Et = [A.f32(1024), A.f32(1024)]
                b_E = bufs("E", 2)
                Pn = [A.bf(1024), A.bf(1024)]
                b_Pn = bufs("Pn", 2)
                PTs = [A.bf(1024), A.bf(1024)]
                b_PT = bufs("PT", 2)
                ast = [A.f32(16), A.f32(16)]
                for s in range(4):
                    j = s % 2
                    for hp in range(2):
                        bk = 4 + hp
                        for hh in range(2):
                            h_ = hp * 2 + hh
                            for dk in range(2):
                                MM(ps[bk][:, hh * 256:(hh + 1) * 256], qxT[:, h_ * 2 + dk, s * 128:(s + 1) * 128],
                                   kxT[:, h_ * 2 + dk, :], dk == 0, dk == 1,
                                   reads=[b_qx[h_ * 2 + dk], b_kx], writes=[b_ps[bk]])
                        P.op("dve", lambda h, bk=bk, j=j, hp=hp: h.tensor_reduce(
                            out=ast[j][:, hp * 2:hp * 2 + 2], in_=ps[bk][:, :].rearrange("p (a b) -> p a b", b=256),
                            axis=AX.X, op=ALU.max), reads=[b_ps[bk]], writes=[b_E[j]])
                        TS(ast[j][:, 4 + hp * 2:6 + hp * 2], ast[j][:, hp * 2:hp * 2 + 2], -1.0 / 16.0, None, ALU.mult, None,
                           [b_E[j]], [b_E[j]])
                        for hh in range(2):
                            h_ = hp * 2 + hh
                            ACT(Et[j][:, h_ * 256:(h_ + 1) * 256], ps[bk][:, hh * 256:(hh + 1) * 256], AF.Exp,
                                [b_ps[bk], b_E[j]], [b_E[j]], bias=ast[j][:, 4 + h_:5 + h_], scale=1.0 / 16.0,
                                accum=ast[j][:, 8 + h_:9 + h_])
                    P.op("dve", lambda h, j=j: h.reciprocal(out=ast[j][:, 12:16], in_=ast[j][:, 8:12]), reads=[b_E[j]], writes=[b_E[j]])
                    for h_ in range(4):
                        TS(Pn[j][:, h_ * 256:(h_ + 1) * 256], Et[j][:, h_ * 256:(h_ + 1) * 256], ast[j][:, 12 + h_:13 + h_], None,
                           ALU.mult, None, [b_E[j]], [b_Pn[j]])
                    for h_ in range(4):
                        for mc in range(2):
                            TR(psb[6][:, (h_ * 2 + mc) * 128:(h_ * 2 + mc + 1) * 128],
                               Pn[j][:, h_ * 256 + mc * 128:h_ * 256 + (mc + 1) * 128], identb,
                               [b_Pn[j], b_cstb], [b_ps[6]], signal=(h_ == 3 and mc == 1))
                    CP(PTs[j], psb[6][:, 0:1024], [b_ps[6]], [b_PT[j]], eng="act")
                    for hp in range(2):
                        bk = 7 if hp == 0 else 6
                        for hh in range(2):
                            h_ = hp * 2 + hh
                            for dv in range(2):
                                col = (hh * 2 + dv) * 128
                                for mc in range(2):
                                    MM(ps[bk][:, col:col + 128], vx[:, mc, h_ * 256 + dv * 128:h_ * 256 + (dv + 1) * 128],
                                       PTs[j][:, (h_ * 2 + mc) * 128:(h_ * 2 + mc + 1) * 128], mc == 0, mc == 1,
                                       reads=[b_vx, b_PT[j]], writes=[b_ps[bk]], signal=(hh == 1 and dv == 1 and mc == 1))
                        CP(ycT[:, hp * 4:(hp + 1) * 4, s * 128:(s + 1) * 128],
                           ps[bk][:, :].rearrange("p (a b) -> p a b", b=128), [b_ps[bk]], b_yc[hp * 4:(hp + 1) * 4], eng="dve")
                P.barrier()
                A.off = r1_mark

                stage(4, [(R0b[:, 8 * T:16 * T], 4096)])
                gz = A.f32(T)
                ga = A.f32(T)
                gb_ = A.f32(T)
                gi = A.f32(T)
                gw = A.f32(T)
                gfl = A.f32(T)
                gon = A.f32(T)
                gsm = A.f32(64)
                b_g = Buf("g")
                tokg = A.f32(64)
                b_tokg = Buf("tokg")
                hmask = A.f32(16)
                P.op("dve", lambda h: h.memset(gon[0:4, :], 1.0), writes=[b_g])
                for h_ in range(4):
                    TS(hmask[0:4, h_ * 4:(h_ + 1) * 4], onesf[0:4, 0:4], identf[0:4, h_:h_ + 1], None, ALU.mult, None,
                       [b_cst], [b_g])
                for which, bk in ((0, 4), (1, 5)):
                    for k in range(KC):
                        MM(ps[bk][0:4, :], wifb[:, k, which * 4:(which + 1) * 4], xb[:, k, :], k == 0, k == KC - 1,
                           reads=[b_wif] + b_xb, writes=[b_ps[bk]])
                bcol = A.f32(8)
                TT(bcol[0:4, 0:4], bif[0:4, 0:4], identf[0:4, 0:4], ALU.mult, [b_bif, b_cst], [b_g])
                P.op("dve", lambda h: h.tensor_reduce(out=bcol[0:4, 4:5], in_=bcol[0:4, 0:4], axis=AX.X, op=ALU.add), reads=[b_g], writes=[b_g])
                TT(bcol[0:4, 0:4], bif[0:4, 4:8], identf[0:4, 0:4], ALU.mult, [b_bif, b_cst, b_g], [b_g])
                P.op("dve", lambda h: h.tensor_reduce(out=bcol[0:4, 5:6], in_=bcol[0:4, 0:4], axis=AX.X, op=ALU.add), reads=[b_g], writes=[b_g])
                ACT(gi[0:4, :], ps[4][0:4, :], AF.Identity, [b_ps[4], b_g], [b_g], bias=bcol[0:4, 4:5])
                ACT(gz[0:4, :], ps[5][0:4, :], AF.Identity, [b_ps[5], b_g], [b_g], bias=bcol[0:4, 5:6])
                TS(ga[0:4, :], gz[0:4, :], -1.0, None, ALU.mult, None, [b_g], [b_g])
                TT(ga[0:4, :], ga[0:4, :], gz[0:4, :], ALU.max, [b_g], [b_g])
                ACT(ga[0:4, :], ga[0:4, :], AF.Exp, [b_g], [b_g], scale=-1.0)
                ACT(ga[0:4, :], ga[0:4, :], AF.Ln, [b_g], [b_g], bias=1.0)
                TS(gz[0:4, :], gz[0:4, :], 0.0, None, ALU.min, None, [b_g], [b_g])
                TT(gz[0:4, :], gz[0:4, :], ga[0:4, :], ALU.subtract, [b_g], [b_g])
                P.op("dve", lambda h: h.tensor_tensor_scan(out=gb_[0:4, :], data0=gon[0:4, :], data1=gz[0:4, :],
                                                           initial=gstate[0:4, 0:1], op0=ALU.mult, op1=ALU.add),
                     reads=[b_g, b_gst], writes=[b_g])
                TT(gi[0:4, :], gi[0:4, :], gb_[0:4, :], ALU.subtract, [b_g], [b_g])
                P.op("dve", lambda h: h.tensor_reduce(out=gsm[0:4, 0:4], in_=gi[0:4, :].rearrange("p (a b) -> p a b", b=128),
                                                      axis=AX.X, op=ALU.max), reads=[b_g], writes=[b_g])
                P.op("dve", lambda h: h.tensor_tensor_scan(out=gsm[0:4, 4:8], data0=gsm[0:4, 0:4], data1=gsm[0:4, 0:4],
                                                           initial=gstate[0:4, 1:2], op0=ALU.max, op1=ALU.max),
                     reads=[b_g, b_gst], writes=[b_g])
                CP(gsm[0:4, 16:17], gstate[0:4, 1:2], [b_gst, b_g], [b_g])
                CP(gsm[0:4, 17:20], gsm[0:4, 4:7], [b_g], [b_g])
                TS(gsm[0:4, 8:12], gsm[0:4, 4:8], -1.0, -LN16, ALU.mult, ALU.add, [b_g], [b_g])
                TS(gsm[0:4, 12:16], gsm[0:4, 4:8], -1.0, None, ALU.mult, None, [b_g], [b_g])
                TT(gsm[0:4, 20:24], gsm[0:4, 16:20], gsm[0:4, 4:8], ALU.subtract, [b_g], [b_g])
                ACT(gsm[0:4, 20:24], gsm[0:4, 20:24], AF.Exp, [b_g], [b_g])
                for c in range(4):
                    ACT(gw[0:4, c * 128:(c + 1) * 128], gi[0:4, c * 128:(c + 1) * 128], AF.Exp, [b_g], [b_g], bias=gsm[0:4, 8 + c:9 + c])
                    ACT(gfl[0:4, c * 128:(c + 1) * 128], gb_[0:4, c * 128:(c + 1) * 128], AF.Exp, [b_g], [b_g],
                        bias=gsm[0:4, 12 + c:13 + c], scale=-1.0)
                CP(gstate[0:4, 0:1], gb_[0:4, T - 1:T], [b_g], [b_gst])
                CP(gstate[0:4, 1:2], gsm[0:4, 7:8], [b_g], [b_gst])
                for h_ in range(4):
                    TT(gsm[0:4, 24 + h_ * 4:28 + h_ * 4], gsm[0:4, 20:24], hmask[0:4, h_ * 4:(h_ + 1) * 4], ALU.mult, [b_g], [b_g])
                for c in range(4):
                    TR(ps[6][:, c * 8:c * 8 + 4], gw[0:4, c * 128:(c + 1) * 128], identf[0:4, 0:4], [b_g, b_cst], [b_ps[6]], signal=False)
                    TR(ps[6][:, c * 8 + 4:c * 8 + 8], gfl[0:4, c * 128:(c + 1) * 128], identf[0:4, 0:4], [b_g, b_cst], [b_ps[6]], signal=False)
                MM(ps[6][:, 32:48], onesf[0:4, :], gsm[0:4, 24:40], True, True, reads=[b_g, b_cst], writes=[b_ps[6]])
                CP(tokg[:, 0:48], ps[6][:, 0:48], [b_ps[6]], [b_tokg])

                qkh = r3(A.bf(4 * T), T)
                b_qk = bufs("qk", 4)
                sgo = r3(A.bf(4 * T), T)
                b_sg = bufs("sg", 4)
                vmh = r3(A.bf(4 * 512), 512)
                b_vm = bufs("vm", 4)
                ztmp = [A.f32(T + 3), A.f32(T + 3)]
                b_zt = bufs("zt", 2)
                cacc = [A.f32(T), A.f32(T)]
                b_ca = bufs("ca", 2)
                ktok = [A.bf(256), A.bf(256)]
                b_kt = bufs("kt", 2)
                PTm = [A.bf(128), A.bf(128)]
                b_PTm = bufs("PTm", 2)
                Cb = [A.bf(1024), A.bf(1024)]
                b_Cb = bufs("Cb", 2)
                nb = [A.bf(2), A.bf(2)]
                hn = [A.bf(512), A.bf(512)]
                b_hn = bufs("hn", 2)
                sst = [A.f32(32), A.f32(32)]
                b_ss = bufs("ss", 2)
                step = 0
                zc = 0
                for h_ in range(4):
                    WS.add(w_in[l, :, OFF_QK + h_ * 256:OFF_QK + (h_ + 1) * 256], KC, 256)
                    WS.add(w_in[l, :, OFF_QK + 1024 + h_ * 256:OFF_QK + 1024 + (h_ + 1) * 256], KC, 256)
                    WS.add(w_in[l, :, OFF_O + h_ * 512:OFF_O + (h_ + 1) * 512], KC, 512)
                    WS.add(w_in[l, :, OFF_VM + h_ * 512:OFF_VM + (h_ + 1) * 512], KC, 512)
                for h_ in range(4):
                    for qk in range(2):
                        wi = WS.get()
                        wv = wview(wi, KC, 256)
                        for dk in range(2):
                            bk = ppbank()
                            gch = qk * 8 + h_ * 2 + dk
                            fm_group(bk, wv, KC, dk * 128, xb, b_xb, b_wr[wi])
                            zj = zc % 2
                            zc += 1
                            zt = ztmp[zj]
                            CP(zt[:, 0:3], halo[:, gch, :], [b_halo[gch]], [b_zt[zj]])
                            ACT(zt[:, 3:T + 3], ps[bk][:, :], AF.Identity, [b_ps[bk], b_pfm], [b_zt[zj]],
                                bias=pfm[:, fmchunk(OFF_QK) + gch:fmchunk(OFF_QK) + gch + 1])
                            CP(halo[:, gch, :], zt[:, T:T + 3], [b_zt[zj]], [b_halo[gch]])
                            ca = cacc[zj]
                            TS(ca, zt[:, 0:T], pfm[:, P_CONVW + gch * 4:P_CONVW + gch * 4 + 1], None, ALU.mult, None,
                               [b_zt[zj], b_pfm], [b_ca[zj]])
                            for j in range(1, 4):
                                STT(ca, zt[:, j:j + T], pfm[:, P_CONVW + gch * 4 + j:P_CONVW + gch * 4 + j + 1], ca,
                                    ALU.mult, ALU.add, [b_zt[zj], b_pfm, b_ca[zj]], [b_ca[zj]])
                            ACT(qkh[:, qk * 2 + dk, :], ca, AF.Silu, [b_ca[zj], b_pfm], [b_qk[qk * 2 + dk]],
                                bias=pfm[:, P_CONVB + gch:P_CONVB + gch + 1])
                        WS.after_get()
                    wi = WS.get()
                    wv = wview(wi, KC, 512)
                    for c in range(4):
                        bk = ppbank()
                        och = h_ * 4 + c
                        fm_group(bk, wv, KC, c * 128, xb, b_xb, b_wr[wi])
                        zj = zc % 2
                        zc += 1
                        ACT(cacc[zj], ps[bk][:, :], AF.Sigmoid, [b_ps[bk], b_pfm], [b_ca[zj]],
                            bias=pfm[:, fmchunk(OFF_O) + och:fmchunk(OFF_O) + och + 1])
                        TS(sgo[:, c, :], cacc[zj], pfm[:, P_MNG + och:P_MNG + och + 1], None, ALU.mult, None,
                           [b_ca[zj], b_pfm], [b_sg[c]])
                    WS.after_get()
                    wi = WS.get()
                    wv = wview(wi, KC, 512)
                    for s in range(4):
                        bk = ppbank()
                        for k in range(KC):
                            MM(ps[bk][:, :], xb[:, k, s * 128:(s + 1) * 128], wv[:, k, :], k == 0, False,
                               reads=[b_wr[wi]] + b_xb, writes=[b_ps[bk]], signal=False)
                        MM(ps[bk][:, :], onesb[0:1, :], browb[0:1, 1024 + h_ * 512:1024 + (h_ + 1) * 512], False, True,
                           reads=[b_cstb, b_brow], writes=[b_ps[bk]])
                        CP(vmh[:, s, :], ps[bk][:, :], [b_ps[bk]], [b_vm[s]], eng="act")
                    WS.after_get()
                    for c in range(4):
                        j = step % 2
                        step += 1
                        cs = slice(c * 128, (c + 1) * 128)
                        wc = tokg[:, c * 8 + h_:c * 8 + h_ + 1]
                        fl = tokg[:, c * 8 + 4 + h_:c * 8 + 5 + h_]
                        wint = tokg[:, 32 + h_ * 4 + c:33 + h_ * 4 + c]
                        ss = sst[j]
                        for dk in range(2):
                            TR(psb[4][:, dk * 128:(dk + 1) * 128], qkh[:, 2 + dk, cs], identb, [b_qk[2 + dk], b_cstb], [b_ps[4]],
                               signal=(dk == 1))
                        TS(ktok[j], psb[4][:, 0:256], wc, None, ALU.mult, None, [b_ps[4], b_tokg], [b_kt[j]])
                        for dk in range(2):
                            MM(ps[5][:, 0:128], qkh[:, 2 + dk, cs], qkh[:, dk, cs], dk == 0, dk == 1,
                               reads=[b_qk[2 + dk], b_qk[dk]], writes=[b_ps[5]])
                        STT(PTm[j], ps[5][:, 0:128], wc, maskf, ALU.mult, ALU.mult, [b_ps[5], b_tokg, b_cst], [b_PTm[j]])
                        TS(Cst[:, h_ * 1024:(h_ + 1) * 1024], Cst[:, h_ * 1024:(h_ + 1) * 1024], wint, None, ALU.mult, None,
                           [b_C[h_], b_tokg], [b_C[h_]])
                        TS(nst[:, h_ * 2:h_ * 2 + 2], nst[:, h_ * 2:h_ * 2 + 2], wint, None, ALU.mult, None,
                           [b_n[h_], b_tokg], [b_n[h_]])
                        CP(Cb[j], Cst[:, h_ * 1024:(h_ + 1) * 1024], [b_C[h_]], [b_Cb[j]], eng="act")
                        CP(nb[j], nst[:, h_ * 2:h_ * 2 + 2], [b_n[h_]], [b_Cb[j]], eng="dve")
                        nbk = 6 + (step % 2)
                        MM(ps[nbk][:, :], PTm[j], vmh[:, c, :], True, False, reads=[b_PTm[j], b_vm[c]], writes=[b_ps[nbk]], signal=False)
                        for dk in range(2):
                            MM(ps[nbk][:, :], qkh[:, dk, cs], Cb[j][:, dk * 512:(dk + 1) * 512], False, dk == 1,
                               reads=[b_qk[dk], b_Cb[j]], writes=[b_ps[nbk]])
                        MM(ps[5][:, 256:257], PTm[j], onesb[:, 0:1], True, False, reads=[b_PTm[j], b_cstb], writes=[b_ps[5]], signal=False)
                        for dk in range(2):
                            MM(ps[5][:, 256:257], qkh[:, dk, cs], nb[j][:, dk:dk + 1], False, dk == 1,
                               reads=[b_qk[dk], b_Cb[j]], writes=[b_ps[5]])
                        TS(ss[:, 14:15], ps[5][:, 256:257], fl, None, ALU.max, None, [b_ps[5], b_tokg], [b_ss[j]])
                        STT(ss[:, 0:1], ps[5][:, 256:257], -1.0, ss[:, 14:15], ALU.mult, ALU.max, [b_ps[5], b_ss[j]], [b_ss[j]])
                        P.op("dve", lambda h, ss=ss: h.reciprocal(out=ss[:, 1:2], in_=ss[:, 0:1]), reads=[b_ss[j]], writes=[b_ss[j]])
                        P.op("dve", lambda h, ss=ss, nbk=nbk: h.bn_stats(out=ss[:, 8:14], in_=ps[nbk][:, :]), reads=[b_ps[nbk]], writes=[b_ss[j]])
                        P.op("dve", lambda h, ss=ss: h.bn_aggr(out=ss[:, 2:4], in_=ss[:, 8:14]), reads=[b_ss[j]], writes=[b_ss[j]])
                        TT(ss[:, 4:5], ss[:, 1:2], ss[:, 1:2], ALU.mult, [b_ss[j]], [b_ss[j]])
                        TS(ss[:, 5:6], ss[:, 3:4], ss[:, 4:5], EPS, ALU.mult, ALU.add, [b_ss[j]], [b_ss[j]])
                        ACT(ss[:, 6:7], ss[:, 5:6], AF.Sqrt, [b_ss[j]], [b_ss[j]])
                        P.op("dve", lambda h, ss=ss: h.reciprocal(out=ss[:, 7:8], in_=ss[:, 6:7]), reads=[b_ss[j]], writes=[b_ss[j]])
                        TT(ss[:, 7:8], ss[:, 7:8], ss[:, 1:2], ALU.mult, [b_ss[j]], [b_ss[j]])
                        TS(hn[j], ps[nbk][:, :], ss[:, 2:3], ss[:, 7:8], ALU.subtract, ALU.mult, [b_ps[nbk], b_ss[j]], [b_hn[j]])
                        for dv in range(4):
                            TR(psb[4][:, 512 + dv * 128:512 + (dv + 1) * 128], hn[j][:, dv * 128:(dv + 1) * 128], identb,
                               [b_hn[j], b_cstb], [b_ps[4]], signal=(dv == 3))
                        TT(ybT[:, h_ * 4:(h_ + 1) * 4, cs], psb[4][:, 512:1024].rearrange("p (a b) -> p a b", b=128),
                           sgo[:, :, cs], ALU.mult, [b_ps[4]] + b_sg, b_yb[h_ * 4:(h_ + 1) * 4])
                        for dk in range(2):
                            kb = dk
                            MM(ps[kb][:, :], ktok[j][:, dk * 128:(dk + 1) * 128], vmh[:, c, :], True, True,
                               reads=[b_kt[j], b_vm[c]], writes=[b_ps[kb]])
                            TT(Cst[:, h_ * 1024 + dk * 512:h_ * 1024 + (dk + 1) * 512], Cst[:, h_ * 1024 + dk * 512:h_ * 1024 + (dk + 1) * 512],
                               ps[kb][:, :], ALU.add, [b_ps[kb], b_C[h_]], [b_C[h_]])
                        for dk in range(2):
                            MM(ps[2][:, dk:dk + 1], ktok[j][:, dk * 128:(dk + 1) * 128], onesb[:, 0:1], True, True,
                               reads=[b_kt[j], b_cstb], writes=[b_ps[2]], signal=(dk == 1))
                        TT(nst[:, h_ * 2:h_ * 2 + 2], nst[:, h_ * 2:h_ * 2 + 2], ps[2][:, 0:2], ALU.add, [b_ps[2], b_n[h_]], [b_n[h_]])
                P.barrier()
                A.off = r1_mark

                stage(5, [(R0b[:, 16 * T:32 * T], 8192)])
                mrg = r3(A.bf(KC * T), T)
                b_mrg = bufs("mrg", KC)
                m1_mark = A.off
                gtmp = [A.f32(T) for _ in range(4)]
                b_gt = bufs("gt", 4)
                acc = [A.f32(T) for _ in range(4)]
                b_acc = bufs("acc", 4)
                ptmp = [A.f32(T), A.f32(T)]
                b_pt = bufs("pt", 2)
                branches = ((w_pa, 8, yaT, b_ya), (w_pb, 16, ybT, b_yb), (w_pc, 8, ycT, b_yc))
                for db in range(4):
                    for br in range(3):
                        WS.add(w_in[l, :, OFF_G + br * D + db * 512:OFF_G + br * D + (db + 1) * 512], KC, 512)
                        WS.add(branches[br][0][l, :, db * 512:(db + 1) * 512], branches[br][1], 512)
                pc = 0
                for db in range(4):
                    for br in range(3):
                        wpj, nk, yT, b_y = branches[br]
                        wi = WS.get()
                        wv = wview(wi, KC, 512)
                        for c in range(4):
                            bk = ppbank()
                            gch = fmchunk(OFF_G) + br * 16 + db * 4 + c
                            fm_group(bk, wv, KC, c * 128, xb, b_xb, b_wr[wi])
                            ACT(gtmp[c], ps[bk][:, :], AF.Sigmoid, [b_ps[bk], b_pfm], [b_gt[c]], bias=pfm[:, gch:gch + 1])
                        WS.after_get()
                        wi = WS.get()
                        wv = wview(wi, nk, 512)
                        for c in range(4):
                            bk = ppbank()
                            fm_group(bk, wv, nk, c * 128, yT, b_y, b_wr[wi])
                            d = db * 4 + c
                            if br == 0:
                                TT(acc[c], gtmp[c], ps[bk][:, :], ALU.mult, [b_gt[c], b_ps[bk]], [b_acc[c]])
                            else:
                                pj = pc % 2
                                pc += 1
                                TT(ptmp[pj], gtmp[c], ps[bk][:, :], ALU.mult, [b_gt[c], b_ps[bk]], [b_pt[pj]])
                                if br == 1:
                                    TT(acc[c], acc[c], ptmp[pj], ALU.add, [b_acc[c], b_pt[pj]], [b_acc[c]])
                                else:
                                    TT(mrg[:, d, :], acc[c], ptmp[pj], ALU.add, [b_acc[c], b_pt[pj]], [b_mrg[d]])
                        WS.after_get()
                P.barrier()
                A.off = m1_mark

                stage(6, [(mrg.rearrange('p a b -> p (a b)'), 8192)])
                def layer_norm(gcol, bcol_, write_bf):
                    sq = [A.f32(T), A.f32(T)]
                    b_sq = bufs("sq", 2)
                    hl = [[A.bf(T) for _ in range(4)] for _ in range(2)]
                    b_hl = bufs("hl", 2)
                    lst = [A.f32(T) for _ in range(3)]
                    b_l = Buf("lnst")
                    for k in range(KC):
                        j = k % 2
                        hi, lo, hi2, lo2 = hl[j]
                        CP(hi, s32[:, k, :], [b_s32[k]], [b_hl[j]], eng="act")
                        TT(lo, s32[:, k, :], hi, ALU.subtract, [b_s32[k], b_hl[j]], [b_hl[j]])
                        ACT(sq[j], s32[:, k, :], AF.Square, [b_s32[k]], [b_sq[j]])
                        CP(hi2, sq[j], [b_sq[j]], [b_hl[j]], eng="act")
                        TT(lo2, sq[j], hi2, ALU.subtract, [b_sq[j], b_hl[j]], [b_hl[j]])
                        MM(ps[4][:, :], meanwb, hi, k == 0, False, reads=[b_meanw, b_hl[j]], writes=[b_ps[4]], signal=False)
                        MM(ps[4][:, :], meanwb, lo, False, k == KC - 1, reads=[b_meanw, b_hl[j]], writes=[b_ps[4]], signal=False)
                        MM(ps[5][:, :], meanwb, hi2, k == 0, False, reads=[b_meanw, b_hl[j]], writes=[b_ps[5]], signal=False)
                        MM(ps[5][:, :], meanwb, lo2, False, k == KC - 1, reads=[b_meanw, b_hl[j]], writes=[b_ps[5]], signal=True)
                    mean, rstd, nmr = lst
                    CP(mean, ps[4][:, :], [b_ps[4]], [b_l], eng="act")
                    TT(rstd, mean, mean, ALU.mult, [b_l], [b_l])
                    TT(rstd, ps[5][:, :], rstd, ALU.subtract, [b_ps[5], b_l], [b_l])
                    TS(rstd, rstd, EPS, None, ALU.add, None, [b_l], [b_l])
                    ACT(rstd, rstd, AF.Sqrt, [b_l], [b_l])
                    P.op("dve", lambda h: h.reciprocal(out=rstd, in_=rstd), reads=[b_l], writes=[b_l])
                    STT(nmr, mean, -1.0, rstd, ALU.mult, ALU.mult, [b_l], [b_l])
                    for k in range(KC):
                        gk = pfm[:, gcol + k:gcol + k + 1]
                        bk_ = pfm[:, bcol_ + k:bcol_ + k + 1]
                        STT(s32[:, k, :], s32[:, k, :], gk, rstd, ALU.mult, ALU.mult, [b_s32[k], b_l, b_pfm], [b_s32[k]])
                        STT(s32[:, k, :], nmr, gk, s32[:, k, :], ALU.mult, ALU.add, [b_s32[k], b_l, b_pfm], [b_s32[k]])
                        if write_bf:
                            ACT(xb[:, k, :], s32[:, k, :], AF.Identity, [b_s32[k], b_pfm], [b_xb[k]], bias=bk_)
                        ACT(s32[:, k, :], s32[:, k, :], AF.Identity, [b_s32[k], b_pfm], [b_s32[k]], bias=bk_)

                for q in range(4):
                    DMA("sp", s32[:, q * 4:(q + 1) * 4, :],
                        src[q * 512:(q + 1) * 512, t0:t0 + T].rearrange("(k p) t -> p k t", p=128),
                        s_x4[q], writes=b_s32[q * 4:(q + 1) * 4])
                for db in range(4):
                    WS.add(w_out[l, :, db * 512:(db + 1) * 512], KC, 512)
                for db in range(4):
                    wi = WS.get()
                    wv = wview(wi, KC, 512)
                    for c in range(4):
                        bk = ppbank()
                        d = db * 4 + c
                        fm_group(bk, wv, KC, c * 128, mrg, b_mrg, b_wr[wi])
                        STT(s32[:, d, :], s32[:, d, :], ALPHA, ps[bk][:, :], ALU.mult, ALU.add, [b_s32[d], b_ps[bk]], [b_s32[d]])
                    WS.after_get()
                layer_norm(P_L1G, P_L1B, True)
                P.barrier()
                A.off = r1_mark

                stage(7, [(R0, 8192)])
                hT = r3(A.bf(44 * T), T)
                b_h = bufs("h", 44)
                stmp = [A.f32(T) for _ in range(4)]
                b_st = bufs("stmp", 4)
                for fb in range(11):
                    WS.add(w_gu[l, :, fb * 512:(fb + 1) * 512], KC, 512)
                    WS.add(w_gu[l, :, DFF + fb * 512:DFF + (fb + 1) * 512], KC, 512)
                for fb in range(11):
                    wig = WS.get()
                    wvg = wview(wig, KC, 512)
                    for c in range(4):
                        bkg = ppbank()
                        fm_group(bkg, wvg, KC, c * 128, xb, b_xb, b_wr[wig])
                        ACT(stmp[c], ps[bkg][:, :], AF.Silu, [b_ps[bkg]], [b_st[c]])
                    WS.after_get()
                    wiu = WS.get()
                    wvu = wview(wiu, KC, 512)
                    for c in range(4):
                        bku = ppbank()
                        fm_group(bku, wvu, KC, c * 128, xb, b_xb, b_wr[wiu])
                        TT(hT[:, fb * 4 + c, :], stmp[c], ps[bku][:, :], ALU.mult, [b_st[c], b_ps[bku]], [b_h[fb * 4 + c]])
                    WS.after_get()
                for db in range(4):
                    for kq in range(4):
                        WS.add(w_down[l, kq * 1408:(kq + 1) * 1408, db * 512:(db + 1) * 512], 11, 512)
                for db in range(4):
                    for kq in range(4):
                        wi = WS.get()
                        wv = wview(wi, 11, 512)
                        for c in range(4):
                            for k in range(11):
                                kk = kq * 11 + k
                                MM(ps[c][:, :], wv[:, k, c * 128:(c + 1) * 128], hT[:, kk, :], kk == 0, kk == 43,
                                   reads=[b_wr[wi], b_h[kk]], writes=[b_ps[c]], signal=(kk == 43 or k == 10))
                        WS.after_get()
                    for c in range(4):
                        d = db * 4 + c
                        STT(s32[:, d, :], s32[:, d, :], ALPHA, ps[c][:, :], ALU.mult, ALU.add, [b_s32[d], b_ps[c]], [b_s32[d]])
                ppc["n"] = 0
                P.barrier()
                A.off = r1_mark
                layer_norm(P_L2G, P_L2B, False)
                for q in range(4):
                    DMA("sp", dst[q * 512:(q + 1) * 512, t0:t0 + T].rearrange("(k p) t -> p k t", p=128),
                        s32[:, q * 4:(q + 1) * 4, :], s_o4[q], reads=b_s32[q * 4:(q + 1) * 4])
                P.barrier()
                A.off = tile_mark
        except _Stop:
            P.barrier()
            A.off = A.words - 8192 - 16
            dst_ = A.f32(8192)
            b_d = Buf("dbgst")
            off = 0
            for ap, n in dbg_list:
                for q in range(0, n, 2048):
                    m = min(2048, n - q)
                    CP(dst_[:, off + q:off + q + m], ap[:, q:q + m], [], [b_d])
                off += n
            DMA("sp", dbg, dst_, s_o, reads=[b_d])
            P.barrier()
        print("ops", P.n_ops, "sems", P.nsem, "arena peak words", A.peak)
        P.emit(nc)
    return nc


def prep_common(inp, L):
    f = np.float32
    b_in = np.asarray(inp["b_in"], f)[:L]
    nonif = np.concatenate([b_in[:, :8192], b_in[:, 8200:]], axis=1)
    bias_fm = nonif.reshape(L, 120, 128).transpose(0, 2, 1)
    convw = np.asarray(inp["m_conv_w"], f)[:L].reshape(L, 4, 16, 128).transpose(0, 3, 2, 1).reshape(L, 128, 64)

    def fm16(a):
        return np.asarray(a, f)[:L].reshape(L, 16, 128).transpose(0, 2, 1)

    prm_fm = np.ascontiguousarray(np.concatenate(
        [bias_fm, convw, fm16(inp["m_conv_b"]), fm16(inp["m_norm_g"]), fm16(inp["ln1_g"]), fm16(inp["ln1_b"]),
         fm16(inp["ln2_g"]), fm16(inp["ln2_b"])], axis=2), f)
    rowcat = np.concatenate([np.asarray(inp["g_ln_g"], f)[:L], np.asarray(inp["g_ln_b"], f)[:L],
                             np.asarray(inp["g_bs"], f)[:L].reshape(L, 1024), b_in[:, 8192:8200]], axis=1)
    prm_bc = np.ascontiguousarray(np.broadcast_to(rowcat[:, None, :], (L, 128, NPB)), f)
    brow = np.ascontiguousarray(np.concatenate([b_in[:, 1024:2048], b_in[:, 4096:6144]], axis=1)[:, None, :], f)
    wsT = np.ascontiguousarray(np.asarray(inp["g_ws"], f)[:L].transpose(0, 3, 1, 2))
    wif = np.ascontiguousarray(np.asarray(inp["w_in"], f)[:L, :, 8192:8200].reshape(L, 16, 128, 8).transpose(0, 2, 1, 3))
    memg = np.ascontiguousarray(np.concatenate([np.asarray(inp["mem_ln_g"], f).reshape(16, 128).T,
                                                np.asarray(inp["mem_ln_b"], f).reshape(16, 128).T], axis=1), f)
    cst = np.zeros((128, 512), f)
    cst[:, 0:128] = np.eye(128, dtype=f)
    cst[:, 128:256] = np.triu(np.ones((128, 128), f))
    cst[:, 256:384] = 1.0
    com = {"prm_fm": prm_fm, "prm_bc": prm_bc, "brow": brow, "wsT": wsT, "wif": wif, "memg": memg, "cst": cst}
    for k in ("w_in", "x_w_kv", "w_pa", "w_pb", "w_pc", "w_out", "w_gu", "w_down"):
        com[k] = np.ascontiguousarray(np.asarray(inp[k], f)[:L])
    return com


_NC_CACHE = {}


def run(inp, L, S, nb, stop=None):
    key = (L, S, stop)
    if key not in _NC_CACHE:
        _NC_CACHE[key] = build(L, S, stop)
    nc = _NC_CACHE[key]
    com = prep_common(inp, L)
    x = np.asarray(inp["x"], np.float32)
    mem = np.asarray(inp["mem"], np.float32)
    in_maps = []
    for b in range(nb):
        m = dict(com)
        m["xfm"] = np.ascontiguousarray(x[b, :S].T)
        m["mem"] = np.ascontiguousarray(mem[b])
        in_maps.append(m)
    res = run_bass_kernel_spmd(nc, in_maps, core_ids=list(range(nb)))
    if stop is not None:
        return res.results[0]["dbg"]
    out = np.stack([np.ascontiguousarray(r["yfm"].T) for r in res.results], axis=0)
    return out.astype(np.float32)


def kernel(**inputs):
    x = inputs["x"]
    return run(inputs, DEPTH, x.shape[1], x.shape[0])
```
